# Optimizing a Trainium2 kernel written in Bass

```python
import math
import jax, jax.numpy as jnp
from jax import lax
import numpy as np

D_MODEL = 4096
BATCH = 2
SEQ = 8192
DEPTH = 4

ATTN_HEADS = 32
ATTN_KV_HEADS = 4
ATTN_HEAD_DIM = 64
ATTN_GROUP = ATTN_HEADS // ATTN_KV_HEADS
WINDOW = 128
BLOCK = 128
ATTN_SCALE = ATTN_HEAD_DIM ** -0.5
NUM_BUCKETS = 32
MAX_DISTANCE = 128
HG_HEADS = 8
HG_EXPAND = 128
HG_HEAD_V = 128
HG_CHUNK = 64
D_FF = 4 * D_MODEL
EPS = 1e-6

ATTN_Q_DIM = ATTN_HEADS * ATTN_HEAD_DIM
ATTN_KV_DIM = ATTN_KV_HEADS * ATTN_HEAD_DIM
HG_K_DIM = HG_HEADS * HG_EXPAND
HG_V_DIM = HG_HEADS * HG_HEAD_V
MIX_DIM = ATTN_Q_DIM + HG_V_DIM
SPLIT_SIZES = (ATTN_Q_DIM, ATTN_KV_DIM, ATTN_KV_DIM, HG_K_DIM, HG_K_DIM, HG_V_DIM, HG_V_DIM, D_MODEL, D_MODEL)
SPLIT_POINTS = tuple(sum(SPLIT_SIZES[:i + 1]) for i in range(len(SPLIT_SIZES) - 1))
IN_DIM = sum(SPLIT_SIZES)

kernel_name = "hybrid_swa_hgrn2_gated_merge"


def rms_norm(x, gain):
    xf = x.astype(jnp.float32)
    y = xf * lax.rsqrt(jnp.mean(xf * xf, axis=-1, keepdims=True) + EPS)
    return (y * gain.astype(jnp.float32)).astype(x.dtype)


def t5_bucket(dist):
    max_exact = NUM_BUCKETS // 2
    d = jnp.maximum(dist, 1).astype(jnp.float32)
    large = max_exact + (jnp.log(d / max_exact) / math.log(MAX_DISTANCE / max_exact)
                         * (NUM_BUCKETS - max_exact)).astype(jnp.int32)
    return jnp.where(dist < max_exact, dist, jnp.minimum(large, NUM_BUCKETS - 1))


def relative_bias_band(rel_bias, n_blocks):
    qi = jnp.arange(BLOCK)[:, None]
    ki = jnp.arange(2 * BLOCK)[None, :]
    dist = qi + BLOCK - ki
    in_window = (dist >= 0) & (dist < WINDOW)
    bucket = t5_bucket(jnp.maximum(dist, 0))
    bias = jnp.transpose(rel_bias[bucket], (2, 0, 1)).astype(jnp.float32)
    blk = jnp.arange(n_blocks)[:, None, None]
    mask = in_window[None] & ((blk > 0) | (ki[None] >= BLOCK))
    return bias, mask


def sliding_window_attention(q, k, v, sinks, bias, mask):
    B, S = q.shape[:2]
    nb = S // BLOCK
    qb = q.reshape(B, nb, BLOCK, ATTN_KV_HEADS, ATTN_GROUP, ATTN_HEAD_DIM)

    def band(t):
        tp = jnp.pad(t, ((0, 0), (BLOCK, 0), (0, 0), (0, 0))).reshape(B, nb + 1, BLOCK, ATTN_KV_HEADS, ATTN_HEAD_DIM)
        return jnp.concatenate([tp[:, :-1], tp[:, 1:]], axis=2)

    kb, vb = band(k), band(v)
    s = jnp.einsum('bnqkgd,bnskd->bnkgqs', qb, kb, preferred_element_type=jnp.float32) * ATTN_SCALE
    s = s + bias.reshape(ATTN_KV_HEADS, ATTN_GROUP, BLOCK, 2 * BLOCK)
    s = jnp.where(mask[None, :, None, None], s, -jnp.inf)
    sink = sinks.astype(jnp.float32).reshape(1, 1, ATTN_KV_HEADS, ATTN_GROUP, 1, 1)
    m = jnp.maximum(jnp.max(s, axis=-1, keepdims=True), sink)
    p = jnp.exp(s - m)
    p = p / (jnp.sum(p, axis=-1, keepdims=True) + jnp.exp(sink - m))
    o = jnp.einsum('bnkgqs,bnskd->bnqkgd', p.astype(v.dtype), vb)
    return o.reshape(B, S, ATTN_Q_DIM)


def hgrn2_chunk_scan(q, k, logf, v):
    B, S, H, DK = q.shape
    DV = v.shape[-1]
    nc = S // HG_CHUNK

    def to_chunks(t):
        return t.reshape(B, nc, HG_CHUNK, H, t.shape[-1]).transpose(1, 0, 3, 2, 4)

    causal = jnp.tril(jnp.ones((HG_CHUNK, HG_CHUNK), dtype=bool))

    def step(state, chunk):
        qc, kc, lc, vc = chunk
        b = jnp.cumsum(lc, axis=2)
        o_inter = jnp.einsum('bhtd,bhde->bhte', qc * jnp.exp(b), state)
        rel = jnp.where(causal[:, :, None], b[:, :, :, None, :] - b[:, :, None, :, :], -jnp.inf)
        scores = jnp.einsum('bhtd,bhtsd,bhsd->bhts', qc, jnp.exp(rel), kc)
        o_intra = jnp.einsum('bhts,bhse->bhte', scores, vc)
        b_last = b[:, :, -1, :]
        k_to_end = kc * jnp.exp(b_last[:, :, None, :] - b)
        new_state = jnp.exp(b_last)[..., None] * state + jnp.einsum('bhsd,bhse->bhde', k_to_end, vc)
        return new_state, o_inter + o_intra

    state0 = jnp.zeros((B, H, DK, DV), jnp.float32)
    _, o = lax.scan(step, state0, (to_chunks(q), to_chunks(k), to_chunks(logf), to_chunks(v)))
    return o.transpose(1, 0, 3, 2, 4).reshape(B, S, H, DV)


def hgrn2(q, f_logit, i, g, lower_bound, norm_gain):
    B, S, _ = q.shape
    qf = jax.nn.silu(q.astype(jnp.float32)).reshape(B, S, HG_HEADS, HG_EXPAND)
    forget = lower_bound + (1.0 - lower_bound) * jax.nn.sigmoid(f_logit.astype(jnp.float32))
    logf = jnp.log(forget).reshape(B, S, HG_HEADS, HG_EXPAND)
    k = (1.0 - forget).reshape(B, S, HG_HEADS, HG_EXPAND)
    v = i.astype(jnp.float32).reshape(B, S, HG_HEADS, HG_HEAD_V)
    o = hgrn2_chunk_scan(qf, k, logf, v)
    o = rms_norm(o, norm_gain) * jax.nn.silu(g.astype(jnp.float32).reshape(B, S, HG_HEADS, HG_HEAD_V))
    return o.reshape(B, S, HG_V_DIM).astype(q.dtype)


def setup_inputs(seed: int = 0) -> dict:
    key = jax.random.key(seed)
    ks = jax.random.split(key, 16)

    def normal(k, shape, scale):
        return jax.random.normal(k, shape, jnp.float32) * scale

    row_scale = jnp.concatenate([jnp.full((ATTN_Q_DIM, 1), ATTN_Q_DIM ** -0.5, jnp.float32),
                                 jnp.full((HG_V_DIM, 1), HG_V_DIM ** -0.5, jnp.float32)], axis=0)
    return {
        "x": normal(ks[0], (BATCH, SEQ, D_MODEL), 1.0),
        "attn_norm_gain": 1.0 + normal(ks[1], (DEPTH, D_MODEL), 0.1),
        "w_in": normal(ks[2], (DEPTH, D_MODEL, IN_DIM), D_MODEL ** -0.5),
        "q_norm_gain": 1.0 + normal(ks[3], (DEPTH, ATTN_HEAD_DIM), 0.1),
        "k_norm_gain": 1.0 + normal(ks[4], (DEPTH, ATTN_HEAD_DIM), 0.1),
        "attn_sinks": normal(ks[5], (DEPTH, ATTN_HEADS), 0.5),
        "rel_bias": normal(ks[6], (NUM_BUCKETS, ATTN_HEADS), 0.5),
        "hgrn_lb_logits": normal(ks[7], (DEPTH, HG_K_DIM), 0.5),
        "hgrn_norm_gain": 1.0 + normal(ks[8], (DEPTH, HG_HEAD_V), 0.1),
        "w_branch": normal(ks[9], (DEPTH, MIX_DIM, D_MODEL), 1.0) * row_scale,
        "w_out": normal(ks[10], (DEPTH, D_MODEL, D_MODEL), D_MODEL ** -0.5),
        "mlp_norm_gain": 1.0 + normal(ks[11], (DEPTH, D_MODEL), 0.1),
        "w_up": normal(ks[12], (DEPTH, D_MODEL, D_FF), D_MODEL ** -0.5),
        "w_down": normal(ks[13], (DEPTH, D_FF, D_MODEL), D_FF ** -0.5),
    }


def reference(x, attn_norm_gain, w_in, q_norm_gain, k_norm_gain, attn_sinks, rel_bias,
              hgrn_lb_logits, hgrn_norm_gain, w_branch, w_out, mlp_norm_gain, w_up, w_down):
    B, S, _ = x.shape
    bias, band_mask = relative_bias_band(rel_bias, S // BLOCK)
    lb_cum = jnp.cumsum(jax.nn.softmax(hgrn_lb_logits.astype(jnp.float32), axis=0), axis=0)
    lower_bounds = lb_cum - lb_cum[:1]
    for l in range(DEPTH):
        h = rms_norm(x, attn_norm_gain[l])
        proj = jnp.einsum('bsd,de->bse', h, w_in[l])
        aq, ak, av, hq, hf, hi, hg, ga, gh = jnp.split(proj, SPLIT_POINTS, axis=-1)
        aq = rms_norm(aq.reshape(B, S, ATTN_HEADS, ATTN_HEAD_DIM), q_norm_gain[l])
        ak = rms_norm(ak.reshape(B, S, ATTN_KV_HEADS, ATTN_HEAD_DIM), k_norm_gain[l])
        av = av.reshape(B, S, ATTN_KV_HEADS, ATTN_HEAD_DIM)
        o_attn = sliding_window_attention(aq, ak, av, attn_sinks[l], bias, band_mask)
        o_hgrn = hgrn2(hq, hf, hi, hg, lower_bounds[l], hgrn_norm_gain[l])
        branch_a = o_attn @ w_branch[l, :ATTN_Q_DIM]
        branch_h = o_hgrn @ w_branch[l, ATTN_Q_DIM:]
        merged = jax.nn.sigmoid(ga) * branch_a + jax.nn.sigmoid(gh) * branch_h
        x = x + merged @ w_out[l]
        h = rms_norm(x, mlp_norm_gain[l])
        x = x + jnp.square(jax.nn.relu(h @ w_up[l])) @ w_down[l]
    return x
```

```python
import math
import numpy as np
import concourse.bass as bass
import concourse.mybir as mybir
from concourse.bass_utils import run_bass_kernel_spmd

F32 = mybir.dt.float32
BF16 = mybir.dt.bfloat16
AF = mybir.ActivationFunctionType
ALU = mybir.AluOpType
EPS = 1e-6


class Cfg:
    def __init__(s, D=4096, DEPTH=4, BATCH=2, SEQ=8192, AH=32, AKV=4, HGH=8, DFF=16384,
                 T=512, NCORES=2, NB=3, FFB=1024, SEGU=64, LD=None):
        s.D, s.DEPTH, s.BATCH, s.SEQ, s.AH, s.AKV, s.HGH, s.DFF = D, DEPTH, BATCH, SEQ, AH, AKV, HGH, DFF
        s.T, s.NCORES, s.NB, s.FFB = T, NCORES, NB, FFB
        s.LD = LD or DEPTH
        s.SEL = s.LD != DEPTH
        s.TOK = BATCH * SEQ // NCORES
        s.CPS = max(1, NCORES // BATCH)
        s.SPC = BATCH // NCORES if NCORES < BATCH else 1
        s.TPS = SEQ // T if NCORES <= BATCH else s.TOK // T
        s.NT = s.TOK // T
        s.KC = D // 128
        s.QD = AH * 64
        s.KVD = AKV * 64
        s.HK = HGH * 128
        s.G = AH // AKV
        assert s.G == 8
        s.QC = s.QD // 128
        s.KCU = min(s.KC, 8)
        s.NV = AKV * 128
        s.IN = s.QD + 2 * s.KVD + 4 * s.HK + 2 * D
        s.o_k = s.QD
        s.o_v = s.QD + s.KVD
        s.o_hq = s.QD + 2 * s.KVD
        s.o_hf = s.o_hq + s.HK
        s.o_hi = s.o_hf + s.HK
        s.o_hg = s.o_hi + s.HK
        s.o_ga = s.o_hg + s.HK
        s.o_gh = s.o_ga + D
        s.HPH = HGH // 2
        s.HIB = s.HPH * 128
        s.NFB = DFF // FFB
        s.FC = FFB // 128
        s.CC = 2048
        s.units = {}
        off = 0
        s.SEG = 524288 * SEGU

        def add(name, a, b):
            nonlocal off
            sz = 128 * a * b
            if off // s.SEG != (off + sz - 1) // s.SEG:
                off = (off // s.SEG + 1) * s.SEG
            s.units[name] = (off, a, b)
            off += 128 * a * b
        for ci in range(s.QC):
            add(('q', ci), s.KC, 128)
        for g in range(AKV):
            add(('k', g), s.KC, 128)
        for kg in range(s.KC // s.KCU):
            add(('v', kg), s.KCU, s.NV)
        for h in range(HGH):
            add(('hq', h), s.KC, 128)
        for h in range(HGH):
            add(('hf', h), s.KC, 128)
        for blk in range(s.HK // s.HIB):
            for kg in range(s.KC // s.KCU):
                add(('hi', blk, kg), s.KCU, s.HIB)
        for h in range(HGH):
            add(('hg', h), s.KC, 128)
        for e in range(s.KC):
            add(('ga', e), s.KC, 128)
            add(('ba', e), s.QC, 128)
            add(('gh', e), s.KC, 128)
            add(('bh', e), HGH, 128)
        for e in range(s.KC):
            add(('out', e), s.KC, 128)
        for fb in range(s.NFB):
            for f in range(s.FC):
                add(('up', fb * s.FC + f), s.KC, 128)
            for e in range(s.KC):
                add(('down', fb, e), s.FC, 128)
        s.wtot = off
        per = s.SEG
        s.wpad = (off + per - 1) // per * per
        s.NSEG = s.wpad // s.SEG
        s.wrows = s.wpad // s.CC
        s.srows = s.wrows // 8
        assert s.srows % 1 == 0
        s.XF = 8 + HGH * 128 + AKV * 128 + s.NV


def _fm(W, KC):
    n = W.shape[1] // 128
    return np.ascontiguousarray(W.reshape(KC, 128, n, 128).transpose(2, 1, 0, 3))


def pack_layer(cfg, w_in, w_branch, w_out, w_up, w_down, flat=None):
    if flat is None:
        flat = np.zeros(cfg.wpad, np.float32)

    def put(name, arr):
        off, a, b = cfg.units[name]
        assert arr.shape == (128, a, b), (name, arr.shape, a, b)
        flat[off:off + 128 * a * b] = arr.reshape(-1)
    KC = cfg.KC
    q = _fm(w_in[:, :cfg.QD], KC)
    for ci in range(cfg.QC):
        put(('q', ci), q[ci])
    for g in range(cfg.AKV):
        wk = w_in[:, cfg.o_k + g * 64: cfg.o_k + (g + 1) * 64]
        put(('k', g), _fm(np.concatenate([wk, wk], 1), KC)[0])
    wv = np.concatenate([np.concatenate([w_in[:, cfg.o_v + g * 64: cfg.o_v + (g + 1) * 64]] * 2, 1)
                         for g in range(cfg.AKV)], 1)
    wv = wv.reshape(KC, 128, cfg.NV)
    for kg in range(KC // cfg.KCU):
        put(('v', kg), np.ascontiguousarray(wv[kg * cfg.KCU:(kg + 1) * cfg.KCU].transpose(1, 0, 2)))
    for nm, o in (('hq', cfg.o_hq), ('hf', cfg.o_hf), ('hg', cfg.o_hg)):
        t = _fm(w_in[:, o:o + cfg.HK], KC)
        for h in range(cfg.HGH):
            put((nm, h), t[h])
    whi = w_in[:, cfg.o_hi:cfg.o_hi + cfg.HK].reshape(KC, 128, cfg.HK)
    for blk in range(cfg.HK // cfg.HIB):
        for kg in range(KC // cfg.KCU):
            put(('hi', blk, kg), np.ascontiguousarray(
                whi[kg * cfg.KCU:(kg + 1) * cfg.KCU, :, blk * cfg.HIB:(blk + 1) * cfg.HIB].transpose(1, 0, 2)))
    ga = _fm(w_in[:, cfg.o_ga:cfg.o_ga + cfg.D], KC)
    gh = _fm(w_in[:, cfg.o_gh:cfg.o_gh + cfg.D], KC)
    ba = _fm(w_branch[:cfg.QD], cfg.QC)
    bh = _fm(w_branch[cfg.QD:], cfg.HGH)
    wo = _fm(w_out, KC)
    for e in range(KC):
        put(('ga', e), ga[e]); put(('ba', e), ba[e]); put(('gh', e), gh[e]); put(('bh', e), bh[e])
        put(('out', e), wo[e])
    up = _fm(w_up, KC)
    for f in range(cfg.DFF // 128):
        put(('up', f), up[f])
    for fb in range(cfg.NFB):
        dn = _fm(w_down[fb * cfg.FFB:(fb + 1) * cfg.FFB], cfg.FC)
        for e in range(KC):
            put(('down', fb, e), dn[e])
    return flat.reshape(cfg.wrows, cfg.CC)


def t5_bucket_np(dist):
    d = np.maximum(dist, 1).astype(np.float32)
    large = 16 + (np.log(d / np.float32(16)) / np.float32(math.log(128 / 16)) * np.float32(16)).astype(np.int32)
    return np.where(dist < 16, dist, np.minimum(large, 31))


def bias_table(cfg, rel_bias):
    k = np.arange(128)[:, None, None]
    kc = np.arange(2)[None, :, None]
    q = np.arange(128)[None, None, :]
    dist = q + 128 - (kc * 128 + k)
    inwin = (dist >= 0) & (dist < 128)
    bucket = t5_bucket_np(np.maximum(dist, 0))
    b = rel_bias[bucket]
    H = cfg.AH
    b = b.reshape(128, 2, 128, cfg.AKV, 4, 2)
    b = b.transpose(0, 1, 5, 3, 4, 2)
    m = np.where(inwin, np.float32(0), np.float32(-30000.0))[:, :, None, None, None, :]
    m = np.broadcast_to(m, b.shape)
    return np.ascontiguousarray(b).astype(np.float32), np.ascontiguousarray(m).astype(np.float32)


def _freeze(f):
    import types
    if f.__closure__ is None:
        return f
    cells = []
    for c in f.__closure__:
        try:
            cells.append(types.CellType(c.cell_contents))
        except ValueError:
            cells.append(c)
    return types.FunctionType(f.__code__, f.__globals__, f.__name__, f.__defaults__, tuple(cells))


class _Op:
    __slots__ = ('eng', 'fn', 'deps', 'sig', 'val', 'dma', 'chan')


class Prog:
    ENGS = ('pe', 'act', 'dve', 'pool', 'sp')

    def __init__(self, nc):
        self.nc = nc
        self.ops = {e: [] for e in self.ENGS}
        self.lastw = {}
        self.readers = {}
        self.chan_cnt = {}

    def op(self, eng, fn, reads=(), writes=(), chan=None):
        o = _Op()
        o.eng, o.fn, o.sig, o.val, o.chan = eng, _freeze(fn), False, None, chan
        o.dma = chan is not None
        deps = []
        for k in reads:
            w = self.lastw.get(k)
            if w is not None:
                deps.append(w)
        for k in writes:
            w = self.lastw.get(k)
            if w is not None:
                deps.append(w)
            r = self.readers.get(k)
            if r:
                deps.extend(r.values())
        seen = set()
        o.deps = []
        for d in deps:
            if id(d) in seen or d is o:
                continue
            seen.add(id(d))
            if (not d.dma) and d.eng == 'pe' and eng == 'pe' and not o.dma:
                continue
            o.deps.append(d)
            d.sig = True
        if o.dma:
            n = self.chan_cnt.get(chan, 0) + 16
            self.chan_cnt[chan] = n
            o.val = n
        for k in reads:
            r = self.readers.setdefault(k, {})
            r[('dma', id(o)) if o.dma else eng] = o
        for k in writes:
            self.lastw[k] = o
            self.readers[k] = {}
        self.ops[eng].append(o)
        return o

    def finalize(self, final_waits):
        nc = self.nc
        sems = {}
        import contextlib
        with contextlib.ExitStack() as st:
            for e in ('pe', 'act', 'dve', 'pool'):
                sems[e] = st.enter_context(nc.semaphore("s_" + e))
            for c in self.chan_cnt:
                sems[c] = st.enter_context(nc.semaphore("c_" + str(c)))
            for e in ('pe', 'act', 'dve', 'pool'):
                n = 0
                for o in self.ops[e]:
                    if o.dma:
                        continue
                    if o.sig:
                        n += 1
                        o.val = n
            block = st.enter_context(nc.Block())

            def run(engname, eng):
                waited = {}
                for o in self.ops[engname]:
                    for d in o.deps:
                        key = d.chan if d.dma else d.eng
                        if waited.get(key, 0) >= d.val:
                            continue
                        waited[key] = d.val
                        eng.wait_ge(sems[key], d.val)
                    ins = o.fn(eng)
                    if o.dma:
                        ins.then_inc(sems[o.chan], 16)
                    elif o.sig:
                        ins.then_inc(sems[engname], 1)
                if engname == 'sp':
                    for c in final_waits:
                        eng.wait_ge(sems[c], self.chan_cnt[c])

            @block.tensor
            def _(e):
                run('pe', e)

            @block.scalar
            def _(e):
                run('act', e)

            @block.vector
            def _(e):
                run('dve', e)

            @block.gpsimd
            def _(e):
                run('pool', e)

            @block.sync
            def _(e):
                run('sp', e)


def build(cfg):
    import contextlib
    nc = bass.Bass("TRN2", target_bir_lowering=False)
    P = Prog(nc)
    D, T, KC, QC, AKV, HGH, AH, NB = cfg.D, cfg.T, cfg.KC, cfg.QC, cfg.AKV, cfg.HGH, cfg.AH, cfg.NB
    DEPTH, TOK, NT, HK, NV, FC = cfg.DEPTH, cfg.TOK, cfg.NT, cfg.HK, cfg.NV, cfg.FC
    NBK = T // 128
    NCK = T // 64
    BW = 2 * 2 * AKV * 4 * 128

    xin = nc.dram_tensor("xT", [D, TOK], F32, kind="ExternalInput")
    w32 = nc.dram_tensor("w", [DEPTH, cfg.wpad], F32, kind="ExternalInput")
    g1d = nc.dram_tensor("g1", [128, DEPTH * KC], F32, kind="ExternalInput")
    g2d = nc.dram_tensor("g2", [128, DEPTH * KC], F32, kind="ExternalInput")
    smd = nc.dram_tensor("small", [128, 8 * DEPTH], F32, kind="ExternalInput")
    skd = nc.dram_tensor("sinks", [128, DEPTH * AH], F32, kind="ExternalInput")
    lbd = nc.dram_tensor("lbl", [128, cfg.LD * HGH], F32, kind="ExternalInput")
    bid = nc.dram_tensor("biasT", [128, BW], F32, kind="ExternalInput")
    mkd = nc.dram_tensor("maskT", [128, BW], F32, kind="ExternalInput")
    cnd = nc.dram_tensor("consts", [128, 4 * 128 + T], F32, kind="ExternalInput")
    yT = nc.dram_tensor("yT", [D, TOK], F32, kind="ExternalOutput")
    wbf = [[nc.dram_tensor("wbf%d_%d" % (l, sg), [cfg.SEG], BF16) for sg in range(cfg.NSEG)] for l in range(DEPTH)]

    with contextlib.ExitStack() as es:
        def sb(name, shape, dt):
            return es.enter_context(nc.sbuf_tensor(name, shape, dt))

        xT = sb("xTt", [128, KC, T], F32)
        hT = sb("hTt", [128, KC, T], BF16)
        ring = sb("ring", [128, NB, 4096], BF16)
        kTd = sb("kTd", [128, AKV, 128 + T], BF16)
        vdup = sb("vdup", [128, 1 + NBK, NV], BF16)
        sqq = sb("sqq", [128, 2, T], BF16)
        rs = sb("rs", [128, T], F32)
        esc = sb("esc", [128, HGH, 3, NCK], F32)
        Sst = sb("Sst", [128, HGH, 128], F32)
        Spb = sb("Spb", [128, 2, 128], BF16)
        ktok = sb("ktok", [128, 2, 128], BF16)
        scm = sb("scm", [128, 2, 128], BF16)
        mk = sb("mk", [128, 8], F32)
        HPH = cfg.HPH
        fixed = [('qT', [128, QC, T], BF16), ('ohT', [128, HGH, T], BF16)]
        phases = {
            'attn': [('biasm', [128, BW], BF16), ('tt', [128, 2, 512], F32), ('pT', [128, 2, 512], BF16), ('den', [128, 512], F32),
                     ('qf', [128, 2, T], F32), ('rs2', [128, 2, T], F32)],
            'hgrn': [('gkT', [128, HPH, T], BF16), ('gqT', [128, HPH, T], BF16), ('go', [128, HPH, T], F32), ('vtok', [128, NBK, HPH * 128], BF16),
                     ('scr', [128, 1, 4, T], F32), ('sgT', [128, 2, T], F32), ('rs2', [128, 2, T], F32)],
            'merge': [('mg', [128, 1, 2, T], F32), ('merged', [128, KC, T], BF16)],
            'mlp': [('hid', [128, 1, FC, T], BF16), ('rl', [128, 2, T], F32)],
        }

        def words(shape, dt):
            n = 1
            for d_ in shape[1:]:
                n *= d_
            return n if dt is F32 else (n + 1) // 2
        fw = sum(words(sh, dt) for _, sh, dt in fixed)
        pw = max(sum(words(sh, dt) for _, sh, dt in lst) for lst in phases.values())
        arena = sb("arena", [128, fw + pw], F32)

        def aview(off, shape, dt):
            w = words(shape, dt)
            v = arena[:, off:off + w]
            if dt is not F32:
                v = v.bitcast(BF16)
            if len(shape) == 2:
                return v
            if len(shape) == 3:
                return v.rearrange("p (a b) -> p a b", a=shape[1])
            return v.rearrange("p (a b c) -> p a b c", a=shape[1], b=shape[2])
        AV = {}
        off_ = 0
        for nm, sh, dt in fixed:
            AV[nm] = aview(off_, sh, dt)
            off_ += words(sh, dt)
        for ph, lst in phases.items():
            o2_ = fw
            for nm, sh, dt in lst:
                AV[(ph, nm)] = aview(o2_, sh, dt)
                o2_ += words(sh, dt)
        qT = AV['qT']
        oaT = qT
        ohT = AV['ohT']
        biasm, tt, pT, den, qf = (AV[('attn', n_)] for n_ in ('biasm', 'tt', 'pT', 'den', 'qf'))
        rs2a = AV[('attn', 'rs2')]
        gkT, gqT, go, vtok, scr, sgT = (AV[('hgrn', n_)] for n_ in ('gkT', 'gqT', 'go', 'vtok', 'scr', 'sgT'))
        rs2h = AV[('hgrn', 'rs2')]
        mg, merged = AV[('merge', 'mg')], AV[('merge', 'merged')]
        hid, rl = AV[('mlp', 'hid')], AV[('mlp', 'rl')]
        biasd = nc.dram_tensor("biasd", [128, BW], BF16)
        cst = sb("cst", [128, 4 * 128 + T], F32)
        cbf = sb("cbf", [128, 4 * 128], BF16)
        g1 = sb("g1t", [128, DEPTH * KC], F32)
        g2 = sb("g2t", [128, DEPTH * KC], F32)
        sm = sb("smt", [128, 8 * DEPTH], F32)
        esk = sb("esk", [128, DEPTH * AH], F32)
        lbt = sb("lbt", [128, cfg.LD * HGH], F32)
        lb = sb("lb", [128, cfg.LD * HGH], F32)
        oml = sb("oml", [128, cfg.LD * HGH], F32)
        lbs = sb("lbs", [128, HGH], F32)
        omls = sb("omls", [128, HGH], F32)
        lsum = sb("lsum", [128, HGH], F32)
        pss = [es.enter_context(nc.psum_tensor("ps%d" % i, [128, 512], F32)) for i in range(6)]
        psm = es.enter_context(nc.psum_tensor("psm", [128, 512], F32))
        pstr = es.enter_context(nc.psum_tensor("pstr", [128, 8, 128], BF16))

        ident, ones, bdiag, mask128 = (cbf[:, i * 128:(i + 1) * 128] for i in range(4))
        scanmask = cst[:, 512:512 + T]
        pctr = [0]

        def psget():
            i = pctr[0] % 6
            pctr[0] += 1
            return pss[i], ('ps', i)
        tctr = [0]

        def ptget():
            i = tctr[0] % 8
            tctr[0] += 1
            return pstr[:, i, :], ('pt', i)

        dctr = [0]

        def dma(out, in_, reads, writes, chan=None):
            if chan is None:
                chan = ('m', dctr[0] % 8)
                dctr[0] += 1
            return P.op('sp', lambda e: e.dma_start(out=out, in_=in_), reads=reads, writes=writes, chan=chan)

        def OP(eng, f, reads, writes):
            return P.op(eng, f, reads=reads, writes=writes)

        BK1 = [('bm', x) for x in ('act', 'dve', 'pool', 'pe')]
        BK2 = [('bm2', x) for x in ('act', 'dve', 'pool', 'pe')]

        def barrier():
            for stage, rd, pref in ((0, [], 'bm'), (1, BK1, 'bm2')):
                OP('act', lambda e: e.activation(out=mk[:, 0:1], in_=mk[:, 4:5], func=AF.Copy), rd + ['mkz'], [(pref, 'act')])
                OP('dve', lambda e: e.memset(mk[:, 1:2], 0.0), rd, [(pref, 'dve')])
                OP('pool', lambda e: e.memset(mk[:, 2:3], 0.0), rd, [(pref, 'pool')])
                OP('pe', lambda e, stage=stage: e.matmul(psm[:, stage:stage + 1], lhsT=ones, rhs=cbf[:, 0:1], start=True, stop=True), rd + ['cbf'], [(pref, 'pe')])

        OP('dve', lambda e: e.memset(mk[:, 4:8], 0.0), [], ['mkz'])
        dma(cst[:, :], cnd.ap(), [], ['cst'])
        dma(g1[:, :], g1d.ap(), [], ['g1'])
        dma(g2[:, :], g2d.ap(), [], ['g2'])
        dma(sm[:, :], smd.ap(), [], ['sm'])
        dma(esk[:, :], skd.ap(), [], ['esk'])
        dma(lbt[:, :], lbd.ap(), [], ['lbt'])
        OP('dve', lambda e: e.tensor_copy(out=cbf[:, :], in_=cst[:, 0:512]), ['cst'], ['cbf'])
        sD = float(math.sqrt(D))
        OP('dve', lambda e: e.tensor_scalar(out=g1[:, :], in0=g1[:, :], scalar1=sD, scalar2=None, op0=ALU.mult), ['g1'], ['g1'])
        OP('dve', lambda e: e.tensor_scalar(out=g2[:, :], in0=g2[:, :], scalar1=sD, scalar2=None, op0=ALU.mult), ['g2'], ['g2'])
        OP('dve', lambda e: e.tensor_scalar(out=sm[:, 0:2 * DEPTH], in0=sm[:, 0:2 * DEPTH], scalar1=8.0, scalar2=None, op0=ALU.mult), ['sm'], ['sm'])
        OP('dve', lambda e: e.tensor_scalar(out=sm[:, 2 * DEPTH:3 * DEPTH], in0=sm[:, 2 * DEPTH:3 * DEPTH], scalar1=float(math.sqrt(128.0)), scalar2=None, op0=ALU.mult), ['sm'], ['sm'])
        OP('act', lambda e: e.activation(out=esk[:, :], in_=esk[:, :], func=AF.Exp), ['esk'], ['esk'])
        OP('act', lambda e: e.activation(out=lbt[:, :], in_=lbt[:, :], func=AF.Exp), ['lbt'], ['lbt'])
        OP('dve', lambda e: e.tensor_copy(out=lsum[:, :], in_=lbt[:, 0:HGH]), ['lbt'], ['lsum'])
        for l in range(1, cfg.LD):
            OP('dve', lambda e, l=l: e.tensor_tensor(out=lsum[:, :], in0=lsum[:, :], in1=lbt[:, l * HGH:(l + 1) * HGH], op=ALU.add), ['lbt', 'lsum'], ['lsum'])
        OP('dve', lambda e: e.reciprocal(out=lsum[:, :], in_=lsum[:, :]), ['lsum'], ['lsum'])
        OP('dve', lambda e: e.memset(lb[:, 0:HGH], 0.0), [], ['lb'])
        for l in range(1, cfg.LD):
            OP('dve', lambda e, l=l: e.tensor_tensor(out=lb[:, l * HGH:(l + 1) * HGH], in0=lbt[:, l * HGH:(l + 1) * HGH], in1=lsum[:, :], op=ALU.mult), ['lbt', 'lsum', 'lb'], ['lb'])
            OP('dve', lambda e, l=l: e.tensor_tensor(out=lb[:, l * HGH:(l + 1) * HGH], in0=lb[:, l * HGH:(l + 1) * HGH], in1=lb[:, (l - 1) * HGH:l * HGH], op=ALU.add), ['lb'], ['lb'])
        OP('dve', lambda e: e.tensor_scalar(out=oml[:, :], in0=lb[:, :], scalar1=-1.0, scalar2=1.0, op0=ALU.mult, op1=ALU.add), ['lb'], ['oml'])
        if cfg.SEL:
            OP('dve', lambda e: e.tensor_scalar(out=lbs[:, :], in0=lb[:, 0:HGH], scalar1=sm[:, 4:5], scalar2=None, op0=ALU.mult), ['lb', 'sm'], ['lbu'])
            for l2 in range(1, cfg.LD):
                OP('dve', lambda e, l2=l2: e.scalar_tensor_tensor(out=lbs[:, :], in0=lb[:, l2 * HGH:(l2 + 1) * HGH], scalar=sm[:, 4 + l2:5 + l2], in1=lbs[:, :], op0=ALU.mult, op1=ALU.add),
                   ['lb', 'sm', 'lbu'], ['lbu'])
            OP('dve', lambda e: e.tensor_scalar(out=omls[:, :], in0=lbs[:, :], scalar1=-1.0, scalar2=1.0, op0=ALU.mult, op1=ALU.add), ['lbu'], ['lbu'])
            lbu, omlu = lbs, omls
        else:
            lbu, omlu = lb, oml
        xflat = xT[:, :, :].rearrange("p c t -> p (c t)")
        assert KC * T >= 2 * BW or True
        PIECE = min(4096, KC * T // 2)
        stg = xflat.rearrange("p (s f) -> p s f", s=2)
        for hh in range(BW // PIECE):
            n = PIECE
            dma(stg[:, 0, 0:n], bid.ap()[:, hh * n:(hh + 1) * n], [], [('stg', 0)])
            dma(stg[:, 1, 0:n], mkd.ap()[:, hh * n:(hh + 1) * n], [], [('stg', 1)])
            OP('dve', lambda e, hh=hh, n=n: e.tensor_tensor(out=ring[:, 0, 0:n], in0=stg[:, 0, 0:n], in1=stg[:, 1, 0:n], op=ALU.add),
               [('stg', 0), ('stg', 1)], [('ring', 0)])
            dma(biasd.ap()[:, hh * n:(hh + 1) * n], ring[:, 0, 0:n], [('ring', 0)], ['biasd'])

        per = cfg.SEG // 128
        assert per % PIECE == 0 and BW % PIECE == 0
        cengs = ['dve', 'act', 'pool']
        cnt = 0
        for l, sg in [(l, sg) for l in range(DEPTH) for sg in range(cfg.NSEG)]:
            src = w32.ap()[l, sg * cfg.SEG:(sg + 1) * cfg.SEG].rearrange("(p f) -> p f", p=128)
            dst = wbf[l][sg].ap().rearrange("(p f) -> p f", p=128)
            for i in range(per // PIECE):
                sl = cnt % 2
                rsl = cnt % NB
                dma(stg[:, sl, 0:PIECE], src[:, i * PIECE:(i + 1) * PIECE], [], [('stg', sl)], chan=('stg', sl))
                eng = cengs[cnt % 3]
                if eng == 'act':
                    OP('act', lambda e, sl=sl, rsl=rsl: e.activation(out=ring[:, rsl, 0:PIECE], in_=stg[:, sl, 0:PIECE], func=AF.Copy), [('stg', sl)], [('ring', rsl)])
                else:
                    OP(eng, lambda e, sl=sl, rsl=rsl: e.tensor_copy(out=ring[:, rsl, 0:PIECE], in_=stg[:, sl, 0:PIECE]), [('stg', sl)], [('ring', rsl)])
                dma(dst[:, i * PIECE:(i + 1) * PIECE], ring[:, rsl, 0:PIECE], [('ring', rsl)], [('wbf', l)], chan=('cvo', rsl))
                cnt += 1

        wcount = [0]

        def wget(l, name):
            off, a, b = cfg.units[name]
            slot = wcount[0] % NB
            wcount[0] += 1
            sg, o2 = off // cfg.SEG, off % cfg.SEG
            src = wbf[l][sg].ap()[o2:o2 + 128 * a * b].rearrange("(p f) -> p f", p=128)
            dst = ring[:, slot, 0:a * b]
            dma(dst, src, [('wbf', l)], [('ring', slot)], chan=('ring', slot))
            return ring[:, slot, 0:a * b].rearrange("p (a b) -> p a b", a=a), ('ring', slot)

        def proj(l, uname, rhs_fn, rhs_key_fn, nk, N=None):
            N = N or T
            wu, wkey = wget(l, uname)
            ps, pk = psget()
            for kc in range(nk):
                rk = rhs_fn(kc)
                OP('pe', lambda e, kc=kc, rk=rk: e.matmul(ps[:, 0:N], lhsT=wu[:, kc, :], rhs=rk, start=(kc == 0), stop=(kc == nk - 1)),
                   [wkey, rhs_key_fn(kc)], [pk])
            return ps, pk

        def rsq(out, src, c, skey, okey):
            OP('dve', lambda e: e.tensor_scalar(out=out, in0=src, scalar1=c, scalar2=None, op0=ALU.add), [skey], [okey])
            OP('act', lambda e: e.activation(out=out, in_=out, func=AF.Sqrt), [okey], [okey])
            OP('dve', lambda e: e.reciprocal(out=out, in_=out), [okey], [okey])

        def norm(gt, l):
            ps, pk = psget()
            for c in range(KC):
                j = c % 2
                OP('act', lambda e, c=c, j=j: e.activation(out=sqq[:, j, :], in_=xT[:, c, :], func=AF.Square), [('x', c)], [('sqq', j)])
                OP('pe', lambda e, c=c, j=j: e.matmul(ps[:, 0:T], lhsT=ones, rhs=sqq[:, j, :], start=(c == 0), stop=(c == KC - 1)), [('sqq', j), 'cbf'], [pk])
            rsq(rs[:, :], ps[:, 0:T], float(D * EPS), pk, 'rs')
            for c in range(KC):
                OP('dve', lambda e, c=c: e.scalar_tensor_tensor(out=hT[:, c, :], in0=xT[:, c, :], scalar=gt[:, l * KC + c:l * KC + c + 1], in1=rs[:, :], op0=ALU.mult, op1=ALU.mult),
                   [('x', c), 'rs', 'g1', 'g2'], [('h', c)])

        hrhs = lambda kc: hT[:, kc, :]
        hkey = lambda kc: ('h', kc)
        nctr = [0]

        def qknorm(ps, pk, out_ap, out_key, gcol):
            j = nctr[0] % 2
            nctr[0] += 1
            OP('act', lambda e: e.activation(out=qf[:, j, :], in_=ps[:, 0:T], func=AF.Copy), [pk], [('qf', j)])
            OP('act', lambda e: e.activation(out=sqq[:, j, :], in_=ps[:, 0:T], func=AF.Square), [pk], [('sqq', j)])
            ps2, pk2 = psget()
            OP('pe', lambda e: e.matmul(ps2[:, 0:T], lhsT=bdiag, rhs=sqq[:, j, :], start=True, stop=True), [('sqq', j), 'cbf'], [pk2])
            rsq(rs2a[:, j, :], ps2[:, 0:T], float(64 * EPS), pk2, ('rs2a', j))
            OP('dve', lambda e: e.scalar_tensor_tensor(out=out_ap, in0=qf[:, j, :], scalar=gcol, in1=rs2a[:, j, :], op0=ALU.mult, op1=ALU.mult),
               [('qf', j), ('rs2a', j), 'sm'], [out_key])

        def attention(l, t):
            dma(biasm[:, :], biasd.ap(), ['biasd'] + BK2, ['biasm'])
            for ci in range(QC):
                ps, pk = proj(l, ('q', ci), hrhs, hkey, KC)
                qknorm(ps, pk, qT[:, ci, :], ('q', ci), sm[:, l:l + 1])
            for g in range(AKV):
                ps, pk = proj(l, ('k', g), hrhs, hkey, KC)
                qknorm(ps, pk, kTd[:, g, 128:128 + T], ('k', g), sm[:, DEPTH + l:DEPTH + l + 1])
            banks = [psget() for _ in range(NBK)]
            nkg = KC // cfg.KCU
            for kg in range(nkg):
                wu, wkey = wget(l, ('v', kg))
                for gi in range(NBK):
                    ps, pk = banks[gi]
                    for k2 in range(cfg.KCU):
                        kc = kg * cfg.KCU + k2
                        OP('pe', lambda e, ps=ps, kc=kc, k2=k2, gi=gi, wu=wu: e.matmul(ps[:, 0:NV], lhsT=hT[:, kc, gi * 128:(gi + 1) * 128], rhs=wu[:, k2, :],
                                                                                      start=(kc == 0), stop=(kc == KC - 1)), [wkey, ('h', kc)], [pk])
            for gi in range(NBK):
                ps, pk = banks[gi]
                OP('act', lambda e, ps=ps, gi=gi: e.activation(out=vdup[:, 1 + gi, :], in_=ps[:, 0:NV], func=AF.Copy), [pk], [('v', 1 + gi)])
            for g in range(AKV):
                for j in range(NBK):
                    chunks = [0, 1]
                    if t % cfg.TPS == 0 and j == 0:
                        chunks = [1]
                    for half in range(2):
                        hs = slice(half * 64, half * 64 + 64)
                        for ki, kc in enumerate(chunks):
                            ps, pk = psget()
                            kcols = slice((j + kc) * 128, (j + kc + 1) * 128)
                            OP('pe', lambda e, ps=ps, kcols=kcols: e.matmul(ps[:, 0:512], lhsT=kTd[hs, g, kcols], rhs=qT[hs, 4 * g:4 * g + 4, j * 128:(j + 1) * 128], start=True, stop=True),
                               [('k', g), ('khalo', g)] + [('q', 4 * g + i) for i in range(4)], [pk])
                            boff = ((kc * 2 + half) * AKV + g) * 512
                            OP('dve', lambda e, ps=ps, ki=ki, boff=boff: e.scalar_tensor_tensor(out=tt[:, ki, :], in0=ps[:, 0:512], scalar=0.125, in1=biasm[:, boff:boff + 512], op0=ALU.mult, op1=ALU.add),
                               [pk, 'biasm'], [('tt', ki)])
                            OP('act', lambda e, ki=ki: e.activation(out=pT[:, ki, :], in_=tt[:, ki, :], func=AF.Exp), [('tt', ki)], [('pT', ki)])
                        pso, pko = psget()
                        psd, pkd = psget()
                        for ki, kc in enumerate(chunks):
                            OP('pe', lambda e, ki=ki, kc=kc: e.matmul(pso[:, 0:512], lhsT=vdup[:, j + kc, g * 128:(g + 1) * 128], rhs=pT[:, ki, :], start=(ki == 0), stop=(ki == len(chunks) - 1)),
                               [('v', j + kc), ('pT', ki)], [pko])
                        for ki, kc in enumerate(chunks):
                            OP('pe', lambda e, ki=ki: e.matmul(psd[:, 0:512], lhsT=ones, rhs=pT[:, ki, :], start=(ki == 0), stop=(ki == len(chunks) - 1)),
                               [('pT', ki), 'cbf'], [pkd])
                        for i in range(4):
                            h = 8 * g + 2 * i + half
                            OP('dve', lambda e, i=i, h=h: e.tensor_scalar(out=den[hs, i * 128:(i + 1) * 128], in0=psd[hs, i * 128:(i + 1) * 128], scalar1=esk[hs, l * AH + h:l * AH + h + 1], scalar2=None, op0=ALU.add),
                               [pkd, 'esk'], ['den'])
                        OP('dve', lambda e: e.reciprocal(out=den[hs, :], in_=den[hs, :]), ['den'], ['den'])
                        for i in range(4):
                            OP('dve', lambda e, i=i: e.tensor_tensor(out=oaT[hs, 4 * g + i, j * 128:(j + 1) * 128], in0=pso[hs, i * 128:(i + 1) * 128], in1=den[hs, i * 128:(i + 1) * 128], op=ALU.mult),
                               [pko, 'den'], [('q', 4 * g + i)])
            for g in range(AKV):
                OP('pool', lambda e, g=g: e.tensor_copy(out=kTd[:, g, 0:128], in_=kTd[:, g, T:T + 128]), [('k', g)], [('khalo', g)])
            OP('pool', lambda e: e.tensor_copy(out=vdup[:, 0, :], in_=vdup[:, NBK, :]), [('v', NBK)], [('v', 0)])

        def hgrn(l, t):
            nkg = KC // cfg.KCU
            for hh in range(2):
                heads = list(range(hh * HPH, (hh + 1) * HPH))
                for h in heads:
                    hl = h - hh * HPH
                    A1, A2, A3, A4 = (scr[:, 0, i, :] for i in range(4))
                    ka = [('scr', i) for i in range(4)]
                    ps, pk = proj(l, ('hf', h), hrhs, hkey, KC)
                    OP('act', lambda e, ps=ps, A1=A1: e.activation(out=A1, in_=ps[:, 0:T], func=AF.Sigmoid), [pk], [ka[0]])
                    c = l * HGH + h
                    OP('dve', lambda e, A1=A1, c=c: e.tensor_scalar(out=A1, in0=A1, scalar1=omlu[:, c:c + 1], scalar2=lbu[:, c:c + 1], op0=ALU.mult, op1=ALU.add), [ka[0], 'lb', 'oml', 'lbu'], [ka[0]])
                    OP('act', lambda e, A1=A1, A2=A2: e.activation(out=A2, in_=A1, func=AF.Ln), [ka[0]], [ka[1]])
                    OP('dve', lambda e, A2=A2, A3=A3: e.tensor_tensor_scan(out=A3, data0=scanmask, data1=A2, initial=0.0, op0=ALU.mult, op1=ALU.add), [ka[1], 'cst'], [ka[2]])
                    A3v = A3.rearrange("p (c s) -> p c s", s=64)
                    OP('dve', lambda e, A3v=A3v, A4=A4: e.tensor_tensor(out=A4.rearrange("p (c s) -> p c s", s=64), in0=A3v, in1=A3v[:, :, 31:32].to_broadcast([128, NCK, 64]), op=ALU.subtract),
                       [ka[2]], [ka[3]])
                    OP('act', lambda e, A2=A2, A4=A4: e.activation(out=A2, in_=A4, func=AF.Exp, scale=-1.0), [ka[3], ka[1]], [ka[1]])
                    OP('act', lambda e, A4=A4: e.activation(out=A4, in_=A4, func=AF.Exp), [ka[3]], [ka[3]])
                    OP('act', lambda e, A3=A3: e.activation(out=A3, in_=A3, func=AF.Exp), [ka[2]], [ka[2]])
                    OP('dve', lambda e, A1=A1: e.tensor_scalar(out=A1, in0=A1, scalar1=-1.0, scalar2=1.0, op0=ALU.mult, op1=ALU.add), [ka[0]], [ka[0]])
                    OP('dve', lambda e, A1=A1, A2=A2, hl=hl: e.tensor_tensor(out=gkT[:, hl, :], in0=A1, in1=A2, op=ALU.mult), [ka[0], ka[1]], [('gk', hl)])
                    A3c = A3.rearrange("p (c s) -> p c s", s=64)
                    A4c = A4.rearrange("p (c s) -> p c s", s=64)
                    OP('dve', lambda e, A3c=A3c, h=h: e.tensor_copy(out=esc[:, h, 0, :], in_=A3c[:, :, 31]), [ka[2]], [('esc', h)])
                    OP('dve', lambda e, A3c=A3c, h=h: e.tensor_copy(out=esc[:, h, 1, :], in_=A3c[:, :, 63]), [ka[2]], [('esc', h)])
                    OP('dve', lambda e, A4c=A4c, h=h: e.tensor_copy(out=esc[:, h, 2, :], in_=A4c[:, :, 63]), [ka[3]], [('esc', h)])
                    ps, pk = proj(l, ('hq', h), hrhs, hkey, KC)
                    OP('act', lambda e, ps=ps, A2=A2: e.activation(out=A2, in_=ps[:, 0:T], func=AF.Silu), [pk, ka[1]], [ka[1]])
                    OP('dve', lambda e, A2=A2, A4=A4, hl=hl: e.tensor_tensor(out=gqT[:, hl, :], in0=A2, in1=A4, op=ALU.mult), [ka[1], ka[3]], [('gq', hl)])
                banks = [psget() for _ in range(NBK)]
                for kg in range(nkg):
                    wu, wkey = wget(l, ('hi', hh, kg))
                    for gi in range(NBK):
                        ps, pk = banks[gi]
                        for k2 in range(cfg.KCU):
                            kc = kg * cfg.KCU + k2
                            OP('pe', lambda e, ps=ps, kc=kc, k2=k2, gi=gi, wu=wu: e.matmul(ps[:, 0:cfg.HIB], lhsT=hT[:, kc, gi * 128:(gi + 1) * 128], rhs=wu[:, k2, :],
                                                                                          start=(kc == 0), stop=(kc == KC - 1)), [wkey, ('h', kc)], [pk])
                for gi in range(NBK):
                    ps, pk = banks[gi]
                    OP('act', lambda e, ps=ps, gi=gi: e.activation(out=vtok[:, gi, :], in_=ps[:, 0:cfg.HIB], func=AF.Copy), [pk], [('vt', gi)])
                sctr = 0
                for gi in range(NBK):
                    gs = slice(gi * 128, (gi + 1) * 128)
                    for h in heads:
                        hl = h - hh * HPH
                        j = sctr % 2
                        sctr += 1
                        pssc, pksc = psget()
                        OP('pe', lambda e, pssc=pssc, hl=hl: e.matmul(pssc[:, 0:128], lhsT=gkT[:, hl, gs], rhs=gqT[:, hl, gs], start=True, stop=True), [('gk', hl), ('gq', hl)], [pksc])
                        OP('dve', lambda e, pssc=pssc, j=j: e.tensor_tensor(out=scm[:, j, :], in0=pssc[:, 0:128], in1=mask128, op=ALU.mult), [pksc, 'cbf'], [('scm', j)])
                        ptt, ptk = ptget()
                        OP('pe', lambda e, ptt=ptt, hl=hl: e.transpose(out=ptt, in_=gkT[:, hl, gs], identity=ident), [('gk', hl), 'cbf'], [ptk])
                        OP('act', lambda e, ptt=ptt, j=j: e.activation(out=ktok[:, j, :], in_=ptt, func=AF.Copy), [ptk], [('ktok', j)])
                        pso, pko = psget()
                        vh = vtok[:, gi, hl * 128:(hl + 1) * 128]
                        OP('pe', lambda e, pso=pso, vh=vh, j=j: e.matmul(pso[:, 0:128], lhsT=vh, rhs=scm[:, j, :], start=True, stop=False), [('vt', gi), ('scm', j)], [pko])
                        for c in range(2):
                            cc = gi * 2 + c
                            rows = slice(c * 64, c * 64 + 64)
                            cs = slice(gi * 128 + c * 64, gi * 128 + c * 64 + 64)
                            jj = c
                            OP('pool', lambda e, h=h, cc=cc, jj=jj: e.tensor_scalar(out=Spb[:, jj, :], in0=Sst[:, h, :], scalar1=esc[:, h, 0, cc:cc + 1], scalar2=None, op0=ALU.mult),
                               [('S', h), ('esc', h)], [('Spb', jj)])
                            OP('pe', lambda e, pso=pso, hl=hl, jj=jj, c=c, cs=cs: e.matmul(pso[:, c * 64:(c + 1) * 64], lhsT=Spb[:, jj, :], rhs=gqT[:, hl, cs], start=False, stop=(c == 1)),
                               [('Spb', jj), ('gq', hl)], [pko])
                            psu, pku = psget()
                            OP('pe', lambda e, psu=psu, j=j, rows=rows, vh=vh: e.matmul(psu[:, 0:128], lhsT=ktok[rows, j, :], rhs=vh[rows, :], start=True, stop=True), [('ktok', j), ('vt', gi)], [pku])
                            OP('dve', lambda e, h=h, cc=cc: e.tensor_scalar(out=Sst[:, h, :], in0=Sst[:, h, :], scalar1=esc[:, h, 1, cc:cc + 1], scalar2=None, op0=ALU.mult), [('S', h), ('esc', h)], [('S', h)])
                            OP('dve', lambda e, psu=psu, h=h, cc=cc: e.scalar_tensor_tensor(out=Sst[:, h, :], in0=psu[:, 0:128], scalar=esc[:, h, 2, cc:cc + 1], in1=Sst[:, h, :], op0=ALU.mult, op1=ALU.add),
                               [pku, ('S', h), ('esc', h)], [('S', h)])
                        OP('act', lambda e, pso=pso, hl=hl: e.activation(out=go[:, hl, gs], in_=pso[:, 0:128], func=AF.Copy), [pko], [('go', hl)])
                for h in heads:
                    hl = h - hh * HPH
                    j = h % 2
                    OP('act', lambda e, hl=hl, j=j: e.activation(out=sqq[:, j, :], in_=go[:, hl, :], func=AF.Square), [('go', hl)], [('sqq', j)])
                    ps2, pk2 = psget()
                    OP('pe', lambda e, ps2=ps2, j=j: e.matmul(ps2[:, 0:T], lhsT=ones, rhs=sqq[:, j, :], start=True, stop=True), [('sqq', j), 'cbf'], [pk2])
                    rsq(rs2h[:, j, :], ps2[:, 0:T], float(128 * EPS), pk2, ('rs2h', j))
                    ps, pk = proj(l, ('hg', h), hrhs, hkey, KC)
                    OP('act', lambda e, ps=ps, j=j: e.activation(out=sgT[:, j, :], in_=ps[:, 0:T], func=AF.Silu), [pk], [('sgT', j)])
                    OP('dve', lambda e, hl=hl, j=j: e.scalar_tensor_tensor(out=rs2h[:, j, :], in0=go[:, hl, :], scalar=sm[:, 2 * DEPTH + l:2 * DEPTH + l + 1], in1=rs2h[:, j, :], op0=ALU.mult, op1=ALU.mult),
                       [('go', hl), ('rs2h', j), 'sm'], [('rs2h', j)])
                    OP('dve', lambda e, h=h, j=j: e.tensor_tensor(out=ohT[:, h, :], in0=rs2h[:, j, :], in1=sgT[:, j, :], op=ALU.mult), [('rs2h', j), ('sgT', j)], [('oh', h)])

        def merge_out(l):
            for eo in range(KC):
                j = 0
                ps, pk = proj(l, ('ga', eo), hrhs, hkey, KC)
                OP('act', lambda e, ps=ps, j=j: e.activation(out=mg[:, j, 0, :], in_=ps[:, 0:T], func=AF.Sigmoid), [pk], [('mg', j, 0)])
                ps, pk = proj(l, ('ba', eo), lambda kc: oaT[:, kc, :], lambda kc: ('q', kc), QC)
                OP('dve', lambda e, ps=ps, j=j: e.tensor_tensor(out=mg[:, j, 0, :], in0=mg[:, j, 0, :], in1=ps[:, 0:T], op=ALU.mult), [pk, ('mg', j, 0)], [('mg', j, 0)])
                ps, pk = proj(l, ('gh', eo), hrhs, hkey, KC)
                OP('act', lambda e, ps=ps, j=j: e.activation(out=mg[:, j, 1, :], in_=ps[:, 0:T], func=AF.Sigmoid), [pk], [('mg', j, 1)])
                ps, pk = proj(l, ('bh', eo), lambda kc: ohT[:, kc, :], lambda kc: ('oh', kc), HGH)
                OP('dve', lambda e, ps=ps, j=j: e.tensor_tensor(out=mg[:, j, 1, :], in0=mg[:, j, 1, :], in1=ps[:, 0:T], op=ALU.mult), [pk, ('mg', j, 1)], [('mg', j, 1)])
                OP('dve', lambda e, j=j, eo=eo: e.tensor_tensor(out=merged[:, eo, :], in0=mg[:, j, 0, :], in1=mg[:, j, 1, :], op=ALU.add), [('mg', j, 0), ('mg', j, 1)], [('mrg', eo)])
            for eo in range(KC):
                ps, pk = proj(l, ('out', eo), lambda kc: merged[:, kc, :], lambda kc: ('mrg', kc), KC)
                OP('dve', lambda e, ps=ps, eo=eo: e.tensor_tensor(out=xT[:, eo, :], in0=xT[:, eo, :], in1=ps[:, 0:T], op=ALU.add), [pk, ('x', eo)], [('x', eo)])

        def mlp(l):
            for fb in range(cfg.NFB):
                hb = 0
                for f in range(FC):
                    j = f % 2
                    ps, pk = proj(l, ('up', fb * FC + f), hrhs, hkey, KC)
                    OP('act', lambda e, ps=ps, j=j: e.activation(out=rl[:, j, :], in_=ps[:, 0:T], func=AF.Relu), [pk], [('rl', j)])
                    OP('dve', lambda e, j=j, f=f: e.tensor_tensor(out=hid[:, hb, f, :], in0=rl[:, j, :], in1=rl[:, j, :], op=ALU.mult), [('rl', j)], [('hid', hb, f)])
                for eo in range(KC):
                    ps, pk = proj(l, ('down', fb, eo), lambda kc: hid[:, hb, kc, :], lambda kc: ('hid', hb, kc), FC)
                    OP('dve', lambda e, ps=ps, eo=eo: e.tensor_tensor(out=xT[:, eo, :], in0=xT[:, eo, :], in1=ps[:, 0:T], op=ALU.add), [pk, ('x', eo)], [('x', eo)])

        xsrc0 = xin.ap().rearrange("(c p) t -> p c t", p=128)
        ydst = yT.ap().rearrange("(c p) t -> p c t", p=128)
        xkeys = [('x', c) for c in range(KC)]
        for l in range(DEPTH):
            for t in range(NT):
                if t % cfg.TPS == 0:
                    for h in range(HGH):
                        OP('dve', lambda e, h=h: e.memset(Sst[:, h, :], 0.0), [], [('S', h)])
                ts = slice(t * T, (t + 1) * T)
                src = xsrc0 if l == 0 else ydst
                dma(xT[:, :, :], src[:, :, ts], [('yd', t)], xkeys + [('stg', 0), ('stg', 1)], chan='xld')
                norm(g1, l)
                barrier()
                attention(l, t)
                barrier()
                hgrn(l, t)
                barrier()
                merge_out(l)
                barrier()
                norm(g2, l)
                mlp(l)
                dma(ydst[:, :, ts], xT[:, :, :], xkeys, [('yd', t)], chan='xst')
        P.finalize(final_waits=['xst'])
    return nc


def host_inputs(cfg, x, attn_norm_gain, w_in, q_norm_gain, k_norm_gain, attn_sinks, rel_bias,
                hgrn_lb_logits, hgrn_norm_gain, w_branch, w_out, mlp_norm_gain, w_up, w_down, sel=None):
    D, KC, DEPTH, T = cfg.D, cfg.KC, cfg.DEPTH, cfg.T
    f32 = np.float32
    w = np.zeros((DEPTH, cfg.wpad), f32)
    for l in range(DEPTH):
        pack_layer(cfg, np.asarray(w_in[l], f32), np.asarray(w_branch[l], f32), np.asarray(w_out[l], f32),
                   np.asarray(w_up[l], f32), np.asarray(w_down[l], f32), flat=w[l])
    g1 = np.ascontiguousarray(np.asarray(attn_norm_gain, f32).reshape(DEPTH, KC, 128).transpose(2, 0, 1).reshape(128, DEPTH * KC))
    g2 = np.ascontiguousarray(np.asarray(mlp_norm_gain, f32).reshape(DEPTH, KC, 128).transpose(2, 0, 1).reshape(128, DEPTH * KC))
    small = np.zeros((128, 8 * DEPTH), f32)
    qg = np.asarray(q_norm_gain, f32)
    kg = np.asarray(k_norm_gain, f32)
    small[:, 0:DEPTH] = np.concatenate([qg, qg], 1).T
    small[:, DEPTH:2 * DEPTH] = np.concatenate([kg, kg], 1).T
    small[:, 2 * DEPTH:3 * DEPTH] = np.asarray(hgrn_norm_gain, f32).T
    sinks = np.ascontiguousarray(np.broadcast_to(np.asarray(attn_sinks, f32).reshape(1, DEPTH * cfg.AH), (128, DEPTH * cfg.AH)))
    lbl = np.ascontiguousarray(np.asarray(hgrn_lb_logits, f32).reshape(cfg.LD, cfg.HGH, 128).transpose(2, 0, 1).reshape(128, cfg.LD * cfg.HGH))
    if sel is not None:
        small[:, 4 + sel] = 1.0
    bT, mT = bias_table(cfg, np.asarray(rel_bias, f32))
    bT = bT.reshape(128, -1)
    mT = mT.reshape(128, -1)
    consts = np.zeros((128, 4 * 128 + T), f32)
    consts[:, 0:128] = np.eye(128, dtype=f32)
    consts[:, 128:256] = 1.0
    consts[0:64, 256:320] = 1.0
    consts[64:128, 320:384] = 1.0
    s = np.arange(128)[:, None]
    tt = np.arange(128)[None, :]
    consts[:, 384:512] = ((s // 64 == tt // 64) & (s <= tt)).astype(f32)
    sm = np.ones(T, f32)
    sm[::64] = 0.0
    consts[:, 512:] = sm[None, :]
    common = dict(w=w, g1=g1, g2=g2, small=small, sinks=sinks, lbl=lbl, biasT=bT, maskT=mT, consts=consts)
    x = np.asarray(x, f32)
    maps = []
    if cfg.NCORES < cfg.BATCH:
        assert cfg.NCORES == 1
        m = dict(common)
        m["xT"] = np.ascontiguousarray(x.reshape(cfg.BATCH * cfg.SEQ, cfg.D).T)
        return [m]
    for c in range(cfg.NCORES):
        b = c // cfg.CPS
        s0 = (c % cfg.CPS) * cfg.TOK
        m = dict(common)
        m["xT"] = np.ascontiguousarray(x[b, s0:s0 + cfg.TOK, :].T)
        maps.append(m)
    return maps


def run_cfg(cfg, **inputs):
    nc = build(cfg)
    maps = host_inputs(cfg, **inputs)
    res = run_bass_kernel_spmd(nc, maps, core_ids=list(range(cfg.NCORES)))
    if cfg.NCORES < cfg.BATCH:
        return np.ascontiguousarray(res.results[0]["yT"].T).reshape(cfg.BATCH, cfg.SEQ, cfg.D)
    out = np.zeros((cfg.BATCH, cfg.SEQ, cfg.D), np.float32)
    for c in range(cfg.NCORES):
        b = c // cfg.CPS
        s0 = (c % cfg.CPS) * cfg.TOK
        out[b, s0:s0 + cfg.TOK, :] = res.results[c]["yT"].T
    return out


def run_layers(cfg, n_layers, **inputs):
    nc = build(cfg)
    x = np.asarray(inputs['x'], np.float32)
    whole = ('x', 'rel_bias', 'hgrn_lb_logits')
    for l in range(n_layers):
        li = {k: (v if k in whole else v[l:l + 1]) for k, v in inputs.items()}
        li['x'] = x
        maps = host_inputs(cfg, sel=l, **li)
        res = run_bass_kernel_spmd(nc, maps, core_ids=list(range(cfg.NCORES)))
        out = np.zeros((cfg.BATCH, cfg.SEQ, cfg.D), np.float32)
        for c in range(cfg.NCORES):
            b = c // cfg.CPS
            s0 = (c % cfg.CPS) * cfg.TOK
            out[b, s0:s0 + cfg.TOK, :] = res.results[c]["yT"].T
        x = out
    return x


def kernel(**inputs):
    return run_layers(Cfg(DEPTH=1, LD=4, NCORES=2), 4, **inputs)
```
